# Optimizing a Trainium2 kernel written in Bass

```python
import math
import jax
import jax.numpy as jnp
from jax import lax
import numpy as np

D_MODEL = 2048
BATCH = 4
SEQ = 2048
DEPTH = 2
DEC_BATCH = 8
DEC_SEQ = 64
PAST_LEN = 4096

CHUNK = 64
N_MIXERS = 2
N_RET = (DEPTH + 1) // 2
N_SSD = DEPTH // 2

RET_HEADS = 8
RET_DK = D_MODEL // RET_HEADS
RET_DV = 2 * RET_DK
RET_QK_DIM = RET_HEADS * RET_DK
RET_V_DIM = RET_HEADS * RET_DV
RET_IN = 2 * RET_QK_DIM + 2 * RET_V_DIM
ROPE_BASE = 10000.0

SSD_D_INNER = 2 * D_MODEL
SSD_HEADDIM = 64
SSD_HEADS = SSD_D_INNER // SSD_HEADDIM
SSD_GROUPS = 8
SSD_HPG = SSD_HEADS // SSD_GROUPS
SSD_DSTATE = 128
CONV_W = 4
CONV_DIM = SSD_D_INNER + 2 * SSD_GROUPS * SSD_DSTATE
SSD_IN = SSD_D_INNER + CONV_DIM + SSD_HEADS

D_FF = -(-8 * D_MODEL // (3 * 256)) * 256

EPS = 1e-6

kernel_name = "retnet_mamba2_streaming_step"


def _rms(xf):
    return xf * lax.rsqrt(jnp.mean(xf * xf, axis=-1, keepdims=True) + EPS)


def rmsnorm(x, w):
    y = _rms(x.astype(jnp.float32)) * w.astype(jnp.float32)
    return y.astype(x.dtype)


def rotary(x, pos):
    half = x.shape[-1] // 2
    inv = ROPE_BASE ** (-jnp.arange(half, dtype=jnp.float32) / half)
    ang = pos.astype(jnp.float32)[:, None] * inv[None, :]
    cos = jnp.cos(ang)[None, :, None, :]
    sin = jnp.sin(ang)[None, :, None, :]
    x1, x2 = x[..., :half], x[..., half:]
    return jnp.concatenate([x1 * cos - x2 * sin, x2 * cos + x1 * sin], axis=-1)


def retention_mixer(h, pos, s0, w_in, w_out):
    f32 = jnp.float32
    b, L, _ = h.shape
    T = min(CHUNK, L)
    nc = L // T
    proj = h @ w_in
    q, k, v, g = jnp.split(proj, [RET_QK_DIM, 2 * RET_QK_DIM, 2 * RET_QK_DIM + RET_V_DIM], axis=-1)
    q = rotary(q.reshape(b, L, RET_HEADS, RET_DK).astype(f32), pos)
    k = rotary(k.reshape(b, L, RET_HEADS, RET_DK).astype(f32), pos) * (RET_DK ** -0.5)
    v = v.reshape(b, L, RET_HEADS, RET_DV).astype(f32)
    log_g = jnp.log1p(-jnp.exp2(-5.0 - jnp.arange(RET_HEADS, dtype=f32)))
    idx = jnp.arange(T, dtype=f32)
    dist = jnp.abs(idx[:, None] - idx[None, :])
    intra_dec = jnp.exp(log_g[:, None, None] * dist)
    q_dec = jnp.exp((idx[:, None] + 1.0) * log_g[None, :])
    k_dec = jnp.exp((T - 1.0 - idx)[:, None] * log_g[None, :])
    c_dec = jnp.exp(T * log_g)
    qc = q.reshape(b, nc, T, RET_HEADS, RET_DK)
    kc = k.reshape(b, nc, T, RET_HEADS, RET_DK)
    vc = v.reshape(b, nc, T, RET_HEADS, RET_DV)
    scores = jnp.einsum("bcihd,bcjhd->bchij", qc, kc) * intra_dec
    y_intra = jnp.einsum("bchij,bcjhe->bcihe", scores, vc)

    def step(S, blk):
        qi, ki, vi = blk
        y_cross = jnp.einsum("bihd,bhde->bihe", qi * q_dec[None, :, :, None], S)
        S = S * c_dec[None, :, None, None] + jnp.einsum(
            "bjhd,bjhe->bhde", ki * k_dec[None, :, :, None], vi)
        return S, y_cross

    s_fin, y_cross = lax.scan(
        step, s0.astype(f32), (qc.swapaxes(0, 1), kc.swapaxes(0, 1), vc.swapaxes(0, 1)))
    y = (y_intra + y_cross.swapaxes(0, 1)).reshape(b, L, RET_HEADS, RET_DV)
    y = _rms(y).reshape(b, L, RET_V_DIM)
    out = (jax.nn.silu(g.astype(f32)) * y).astype(h.dtype) @ w_out
    return out, s_fin.astype(h.dtype)


def ssd_mixer(h, conv0, s0, w_in, conv_w, conv_b, dt_bias, a_log, d_skip, norm_w, w_out):
    f32 = jnp.float32
    b, L, _ = h.shape
    T = min(CHUNK, L)
    nc = L // T
    G, R, P, N = SSD_GROUPS, SSD_HPG, SSD_HEADDIM, SSD_DSTATE
    proj = h @ w_in
    z, xbc, dt = jnp.split(proj, [SSD_D_INNER, SSD_D_INNER + CONV_DIM], axis=-1)
    xpad = jnp.concatenate([conv0.astype(xbc.dtype), xbc], axis=1)
    new_conv = xpad[:, L:]
    xc = conv_b + sum(xpad[:, t:t + L] * conv_w[t] for t in range(CONV_W))
    xc = jax.nn.silu(xc.astype(f32))
    xs, Bm, Cm = jnp.split(xc, [SSD_D_INNER, SSD_D_INNER + G * N], axis=-1)
    x = xs.reshape(b, nc, T, G, R, P)
    Bm = Bm.reshape(b, nc, T, G, N)
    Cm = Cm.reshape(b, nc, T, G, N)
    dt = jax.nn.softplus(dt.astype(f32) + dt_bias.astype(f32))
    A = -jnp.exp(a_log.astype(f32))
    cum = jnp.cumsum((dt * A).reshape(b, nc, T, G, R), axis=2)
    xdt = x * dt.reshape(b, nc, T, G, R)[..., None]
    causal = jnp.tril(jnp.ones((T, T), dtype=bool))[:, :, None, None]
    seg = cum[:, :, :, None] - cum[:, :, None, :]
    Lmat = jnp.exp(jnp.where(causal, seg, -jnp.inf))
    cb = jnp.einsum("bcign,bcjgn->bcijg", Cm, Bm)
    y_intra = jnp.einsum("bcijgr,bcjgrp->bcigrp", cb[..., None] * Lmat, xdt)

    def step(S, blk):
        Ci, Bi, xdti, cumi = blk
        y_cross = jnp.einsum("bign,bgrpn->bigrp", Ci, S) * jnp.exp(cumi)[..., None]
        dec_end = jnp.exp(cumi[:, -1:] - cumi)
        S = S * jnp.exp(cumi[:, -1])[..., None, None] + jnp.einsum(
            "bjgn,bjgrp->bgrpn", Bi, xdti * dec_end[..., None])
        return S, y_cross

    s_fin, y_cross = lax.scan(
        step, s0.astype(f32).reshape(b, G, R, P, N),
        (Cm.swapaxes(0, 1), Bm.swapaxes(0, 1), xdt.swapaxes(0, 1), cum.swapaxes(0, 1)))
    y = y_intra + y_cross.swapaxes(0, 1) + x * d_skip.astype(f32).reshape(G, R)[:, :, None]
    y = y.reshape(b, L, SSD_D_INNER) * jax.nn.silu(z.astype(f32))
    y = _rms(y.reshape(b, L, G, SSD_D_INNER // G)).reshape(b, L, SSD_D_INNER) * norm_w.astype(f32)
    out = y.astype(h.dtype) @ w_out
    return out, s_fin.reshape(b, SSD_HEADS, P, N).astype(h.dtype), new_conv


def swiglu(h, w_gate, w_up, w_down):
    return (jax.nn.silu(h @ w_gate) * (h @ w_up)) @ w_down


def setup_inputs(seed: int = 0) -> dict:
    key = jax.random.key(seed)
    ks = jax.random.split(key, 24)
    f32 = jnp.float32
    nrm = lambda k, shape, scale: jax.random.normal(k, shape, f32) * scale
    dt0 = jnp.exp(jax.random.uniform(ks[13], (N_SSD, SSD_HEADS), f32)
                  * (math.log(0.1) - math.log(0.001)) + math.log(0.001))
    return {
        "x_prompt": nrm(ks[0], (BATCH, SEQ, D_MODEL), 1.0),
        "x_sample": nrm(ks[1], (DEC_BATCH, DEC_SEQ, D_MODEL), 1.0),
        "state_ret": nrm(ks[2], (N_RET, DEC_BATCH, RET_HEADS, RET_DK, RET_DV), 0.5),
        "state_ssm": nrm(ks[3], (N_SSD, DEC_BATCH, SSD_HEADS, SSD_HEADDIM, SSD_DSTATE), 0.5),
        "state_conv": nrm(ks[4], (N_SSD, DEC_BATCH, CONV_W - 1, CONV_DIM), 1.0),
        "ln_mix": 1.0 + nrm(ks[5], (DEPTH, D_MODEL), 0.01),
        "ln_ffn": 1.0 + nrm(ks[6], (DEPTH, D_MODEL), 0.01),
        "ln_final": 1.0 + nrm(ks[7], (D_MODEL,), 0.01),
        "ret_w_in": nrm(ks[8], (N_RET, D_MODEL, RET_IN), D_MODEL ** -0.5),
        "ret_w_out": nrm(ks[9], (N_RET, RET_V_DIM, D_MODEL), RET_V_DIM ** -0.5),
        "ssd_w_in": nrm(ks[10], (N_SSD, D_MODEL, SSD_IN), D_MODEL ** -0.5),
        "ssd_conv_w": nrm(ks[11], (N_SSD, CONV_W, CONV_DIM), CONV_W ** -0.5),
        "ssd_conv_b": nrm(ks[12], (N_SSD, CONV_DIM), 0.01),
        "ssd_dt_bias": dt0 + jnp.log(-jnp.expm1(-dt0)),
        "ssd_a_log": jnp.log(jax.random.uniform(ks[14], (N_SSD, SSD_HEADS), f32, 1.0, 16.0)),
        "ssd_d": 1.0 + nrm(ks[15], (N_SSD, SSD_HEADS), 0.1),
        "ssd_norm_w": 1.0 + nrm(ks[16], (N_SSD, SSD_D_INNER), 0.01),
        "ssd_w_out": nrm(ks[17], (N_SSD, SSD_D_INNER, D_MODEL), SSD_D_INNER ** -0.5),
        "ffn_w_gate": nrm(ks[18], (DEPTH, D_MODEL, D_FF), D_MODEL ** -0.5),
        "ffn_w_up": nrm(ks[19], (DEPTH, D_MODEL, D_FF), D_MODEL ** -0.5),
        "ffn_w_down": nrm(ks[20], (DEPTH, D_FF, D_MODEL), D_FF ** -0.5),
    }


def reference(x_prompt, x_sample, state_ret, state_ssm, state_conv, ln_mix, ln_ffn, ln_final,
              ret_w_in, ret_w_out, ssd_w_in, ssd_conv_w, ssd_conv_b, ssd_dt_bias, ssd_a_log,
              ssd_d, ssd_norm_w, ssd_w_out, ffn_w_gate, ffn_w_up, ffn_w_down):
    b_p, L_p, _ = x_prompt.shape
    L_s = x_sample.shape[1]
    pos_p = jnp.arange(L_p)
    pos_s = PAST_LEN + jnp.arange(L_s)
    xp, xs = x_prompt, x_sample
    ret_p, ret_s, ssm_p, ssm_s, conv_p, conv_s = [], [], [], [], [], []
    for i in range(DEPTH):
        j = i // N_MIXERS
        hp = rmsnorm(xp, ln_mix[i])
        hs = rmsnorm(xs, ln_mix[i])
        if i % N_MIXERS == 0:
            zero_ret = jnp.zeros((b_p, RET_HEADS, RET_DK, RET_DV), xp.dtype)
            yp, sp = retention_mixer(hp, pos_p, zero_ret, ret_w_in[j], ret_w_out[j])
            ys, ss = retention_mixer(hs, pos_s, state_ret[j], ret_w_in[j], ret_w_out[j])
            ret_p.append(sp)
            ret_s.append(ss)
        else:
            zero_conv = jnp.zeros((b_p, CONV_W - 1, CONV_DIM), xp.dtype)
            zero_ssm = jnp.zeros((b_p, SSD_HEADS, SSD_HEADDIM, SSD_DSTATE), xp.dtype)
            yp, sp, cp = ssd_mixer(hp, zero_conv, zero_ssm, ssd_w_in[j], ssd_conv_w[j], ssd_conv_b[j],
                                   ssd_dt_bias[j], ssd_a_log[j], ssd_d[j], ssd_norm_w[j], ssd_w_out[j])
            ys, ss, cs = ssd_mixer(hs, state_conv[j], state_ssm[j], ssd_w_in[j], ssd_conv_w[j],
                                   ssd_conv_b[j], ssd_dt_bias[j], ssd_a_log[j], ssd_d[j],
                                   ssd_norm_w[j], ssd_w_out[j])
            ssm_p.append(sp)
            ssm_s.append(ss)
            conv_p.append(cp)
            conv_s.append(cs)
        xp = xp + yp
        xs = xs + ys
        xp = xp + swiglu(rmsnorm(xp, ln_ffn[i]), ffn_w_gate[i], ffn_w_up[i], ffn_w_down[i])
        xs = xs + swiglu(rmsnorm(xs, ln_ffn[i]), ffn_w_gate[i], ffn_w_up[i], ffn_w_down[i])
    y_prompt = rmsnorm(xp, ln_final)
    y_sample = rmsnorm(xs, ln_final)
    return (y_prompt, y_sample, jnp.stack(ret_p), jnp.stack(ssm_p), jnp.stack(conv_p),
            jnp.stack(ret_s), jnp.stack(ssm_s), jnp.stack(conv_s))
```

```python
import contextlib
import math
import numpy as np
import ml_dtypes
import concourse.bass as bass
import concourse.mybir as mybir
from concourse.bass_utils import run_bass_kernel_spmd

F32 = mybir.dt.float32
BF16 = mybir.dt.bfloat16
ALU = mybir.AluOpType
AF = mybir.ActivationFunctionType

D = 2048
KC = 16
SEQ = 2048
DEC_SEQ = 64
H = 8
DK = 256
DV = 512
RET_IN = 12288
DFF = 5632
NG = 8
SSD_IN = 10304
CONV_DIM = 6144
EPS = 1e-6

ENGINES = ("pe", "act", "dve", "pool", "sp")
N_DMA_SLOTS = 8


class Op:
    __slots__ = ("eng", "fn", "reads", "writes", "dma", "src", "seq", "waits", "signal", "slot_prev")

    def __init__(self, eng, fn, reads, writes, dma):
        self.eng = eng
        self.fn = fn
        self.reads = reads
        self.writes = writes
        self.dma = dma
        self.waits = None
        self.signal = False
        self.slot_prev = None


class Sched:
    def __init__(self):
        self.ops = []

    def add(self, eng, fn, reads=(), writes=(), dma=False):
        op = Op(eng, fn, tuple(reads), tuple(writes), dma)
        self.ops.append(op)
        return op

    def analyze(self):
        eng_count = {e: 0 for e in ENGINES}
        dma_count = {}
        slot_last = {}
        slot_seq = {}
        for op in self.ops:
            if op.dma:
                n = dma_count.get(op.eng, 0)
                dma_count[op.eng] = n + 1
                slot = (op.eng, n % N_DMA_SLOTS)
                op.src = slot
                op.seq = slot_seq.get(slot, 0) + 1
                slot_seq[slot] = op.seq
                op.slot_prev = slot_last.get(slot)
                slot_last[slot] = op
                op.signal = True
            else:
                eng_count[op.eng] += 1
                op.src = op.eng
                op.seq = eng_count[op.eng]
        clock = {e: {} for e in ENGINES}
        done_clock = {}
        last_w = {}
        readers = {}
        for op in self.ops:
            e = op.eng
            deps = {}
            for k in op.reads:
                o = last_w.get(k)
                if o is not None:
                    cur = deps.get(o.src)
                    if cur is None or cur.seq < o.seq:
                        deps[o.src] = o
            for k in op.writes:
                o = last_w.get(k)
                if o is not None and not (o.eng == e and not o.dma and not op.dma):
                    cur = deps.get(o.src)
                    if cur is None or cur.seq < o.seq:
                        deps[o.src] = o
                for o2 in readers.get(k, ()):
                    if not (o2.eng == e and not o2.dma and not op.dma):
                        cur = deps.get(o2.src)
                        if cur is None or cur.seq < o2.seq:
                            deps[o2.src] = o2
            if op.dma and op.slot_prev is not None:
                o = op.slot_prev
                cur = deps.get(o.src)
                if cur is None or cur.seq < o.seq:
                    deps[o.src] = o
            myclock = clock[e]
            waits = []
            for src, o in deps.items():
                if src == "pe" and e == "pe" and not op.dma:
                    continue
                if myclock.get(src, 0) >= o.seq:
                    continue
                waits.append(o)
            for o in waits:
                o.signal = True
                for s, v in done_clock[id(o)].items():
                    if myclock.get(s, 0) < v:
                        myclock[s] = v
            op.waits = waits
            dc = dict(myclock)
            if dc.get(op.src, 0) < op.seq:
                dc[op.src] = op.seq
            done_clock[id(op)] = dc
            for k in op.writes:
                last_w[k] = op
                readers[k] = []
            for k in op.reads:
                lst = readers.get(k)
                if lst is None:
                    readers[k] = [op]
                else:
                    for i, o2 in enumerate(lst):
                        if o2.src == op.src:
                            lst[i] = op
                            break
                    else:
                        lst.append(op)

    def emit(self, nc):
        self.analyze()
        with contextlib.ExitStack() as st:
            sems = {}
            for e in ENGINES:
                sems[e] = st.enter_context(nc.semaphore("s_" + e))
            qs = sorted({op.eng for op in self.ops if op.dma})
            for q in qs:
                for s in range(N_DMA_SLOTS):
                    sems[(q, s)] = st.enter_context(nc.semaphore("d_%s%d" % (q, s)))
            rank = {}
            cnt = {e: 0 for e in ENGINES}
            for op in self.ops:
                if op.dma:
                    rank[id(op)] = 16 * op.seq
                elif op.signal:
                    cnt[op.eng] += 1
                    rank[id(op)] = cnt[op.eng]
            self.sem_counts = dict(cnt)
            per_eng = {e: [] for e in ENGINES}
            for op in self.ops:
                per_eng[op.eng].append(op)
            slot_final = {}
            for op in self.ops:
                if op.dma:
                    slot_final[op.src] = 16 * op.seq
            block = st.enter_context(nc.Block())

            def run(engname, eng):
                for op in per_eng[engname]:
                    for o in op.waits:
                        eng.wait_ge(sems[o.src], rank[id(o)])
                    ins = op.fn(eng)
                    if op.signal:
                        ins.then_inc(sems[op.src], 16 if op.dma else 1)
                for src, v in slot_final.items():
                    if src[0] == engname:
                        eng.wait_ge(sems[src], v)

            @block.tensor
            def _(eng):
                run("pe", eng)

            @block.scalar
            def _(eng):
                run("act", eng)

            @block.vector
            def _(eng):
                run("dve", eng)

            @block.gpsimd
            def _(eng):
                run("pool", eng)

            @block.sync
            def _(eng):
                run("sp", eng)


UNITS = [
    [("p", 0), ("p", 1), ("p", 2), ("p", 3), ("p", 4), ("s", 0)],
    [("p", i) for i in range(5, 11)],
    [("p", i) for i in range(11, 16)],
]
NT_MAX = 6
DBG = {"units": [0, 1, 2], "phases": ("ret", "ffn0", "ssd", "ffn1"), "ncores": 8, "ssd_parts": ("dt", "x", "B", "C", "z", "scan")}
NWBUF = 2


def ntiles_of(tiles):
    npr = sum(1 for k, _ in tiles if k == "p") * 128
    out = []
    c = 0
    while c < npr:
        nn = min(512, npr - c)
        out.append((c, nn))
        c += nn
    if len(tiles) * 128 > npr:
        out.append((npr, 128))
    return out


def gammas():
    return [1.0 - 2.0 ** (-5.0 - h) for h in range(H)]


def host_consts(u):
    tiles = UNITS[u]
    nt = len(tiles)
    pos = np.zeros(nt * 128, np.float64)
    for t, (k, i) in enumerate(tiles):
        if k == "p":
            pos[t * 128:(t + 1) * 128] = i * 128 + np.arange(128)
        else:
            pos[t * 128:(t + 1) * 128] = 4096 + np.arange(128)
    half = DK // 2
    inv = 10000.0 ** (-np.arange(half, dtype=np.float32) / half)
    ang = pos.astype(np.float32)[None, :] * inv.astype(np.float32)[:, None]
    cos = np.zeros((128, NT_MAX * 128), np.float32)
    sin = np.zeros((128, NT_MAX * 128), np.float32)
    cos[:, :nt * 128] = np.cos(ang.astype(np.float64))
    sin[:, :nt * 128] = np.sin(ang.astype(np.float64))
    return cos, sin


def host_tables():
    g = np.array(gammas(), np.float64)
    lg = np.log(g)
    j = np.arange(128)[:, None]
    i = np.arange(128)[None, :]
    cj, ci = j // 64, i // 64
    mask = np.zeros((128, H, 128), np.float64)
    for h in range(H):
        m = np.where(cj == ci, np.exp(lg[h] * np.abs(i - j)),
                     np.where(cj < ci, np.exp(lg[h] * (i - j)), 0.0))
        mask[:, h, :] = m / 16.0
    qdec = np.zeros((128, H, 128), np.float64)
    for h in range(H):
        qdec[:, h, :] = np.exp(lg[h] * (np.arange(128) + 1.0))[None, :]
    kdec = np.zeros((128, 2, H), np.float64)
    for h in range(H):
        kdec[:, 0, h] = np.exp(lg[h] * (127.0 - np.arange(128))) / 16.0
        ks = np.exp(lg[h] * (63.0 - np.arange(128))) / 16.0
        ks[64:] = 0.0
        kdec[:, 1, h] = ks
    U = (j <= i).astype(np.float32)
    return (mask.astype(np.float32), qdec.astype(np.float32), kdec.astype(np.float32), U)


def build_program():
    nc = bass.Bass("TRN2", target_bir_lowering=False)
    S = Sched()

    def din(name, shape, dt=F32):
        return nc.dram_tensor(name, list(shape), dt, kind="ExternalInput").ap()

    def dout(name, shape):
        return nc.dram_tensor(name, list(shape), F32, kind="ExternalOutput").ap()

    def dscr(name, shape):
        return nc.dram_tensor(name, list(shape), F32).ap()

    xin = din("xin", [SEQ + DEC_SEQ, D])
    st_ret = din("st_ret", [H, DK, DV])
    st_ssm = din("st_ssm", [64 * 64, 128])
    st_conv = din("st_conv", [CONV_DIM, 3])
    wcol_h = din("wcol_h", [128, 5, KC])
    nwcol_h = din("nwcol_h", [128, 32])
    cw_h = din("cw_h", [128, 48, 4])
    cb_h = din("cb_h", [128, 48])
    dtb_h = din("dtb_h", [128, 64])
    alog_h = din("alog_h", [128, 64])
    dsk_h = din("dsk_h", [128, 64])
    lnf_h = din("lnf_h", [128, D])
    ret_w_in = din("ret_w_in", [D, RET_IN])
    ret_w_out = din("ret_w_out", [4096, D])
    ssd_w_in = din("ssd_w_in", [D, SSD_IN])
    ssd_w_out = din("ssd_w_out", [4096, D])
    w_gate = din("w_gate", [2, D, DFF])
    w_up = din("w_up", [2, D, DFF])
    w_down = din("w_down", [2, DFF, D])
    c_identb = din("c_identb", [128, 128], BF16)
    c_identf = din("c_identf", [128, 128])
    c_U = din("c_U", [128, 128])
    c_ones = din("c_ones", [128, 128])
    c_Ub = din("c_Ub", [128, 128], BF16)
    c_onesb = din("c_onesb", [128, 128], BF16)
    c_mask = din("c_mask", [128, H, 128])
    c_qdec = din("c_qdec", [128, H, 128])
    c_kdec = din("c_kdec", [128, 2, H])
    c_cos = din("c_cos", [3, 128, NT_MAX * 128])
    c_sin = din("c_sin", [3, 128, NT_MAX * 128])

    yout = dout("yout", [SEQ + DEC_SEQ, D])
    o_ret_p = dout("o_ret_p", [H, DK, DV])
    o_ret_s = dout("o_ret_s", [H, DK, DV])
    o_ssm_p = dout("o_ssm_p", [64 * 64, 128])
    o_ssm_s = dout("o_ssm_s", [64 * 64, 128])
    o_conv_p = dout("o_conv_p", [CONV_DIM, 3])
    o_conv_s = dout("o_conv_s", [CONV_DIM, 3])

    scr_ret = dscr("scr_ret", [H, DK, DV])
    scr_ssm = dscr("scr_ssm", [NG, 128, 512])
    scr_conv = dscr("scr_conv", [CONV_DIM, 3])

    gam = gammas()

    with contextlib.ExitStack() as st:
        def sb(name, shape, dt):
            return st.enter_context(nc.sbuf_tensor(name, list(shape), dt))

        NTT = NT_MAX * 128
        xres = sb("xres", [128, NT_MAX, D], F32)
        hT = sb("hT", [128, KC, NTT], BF16)
        wbuf = [sb("wbuf%d" % i, [128, 8192], BF16) for i in range(NWBUF)]
        identb = sb("identb", [128, 128], BF16)
        identf = sb("identf", [128, 128], F32)
        Umat = sb("Umat", [128, 128], F32)
        ones = sb("ones", [128, 128], F32)
        Ub = sb("Ub", [128, 128], BF16)
        onesb = sb("onesb", [128, 128], BF16)
        rmask = sb("rmask", [128, H, 128], F32)
        qdec = sb("qdec", [128, H, 128], F32)
        kdec = sb("kdec", [128, 2, H], F32)
        cosT = sb("cosT", [128, NTT], F32)
        sinT = sb("sinT", [128, NTT], F32)
        wcol = sb("wcol", [128, 5, KC], F32)
        nwcol = sb("nwcol", [128, 32], F32)
        cw = sb("cw", [128, 48, 4], F32)
        cb = sb("cb", [128, 48], F32)
        dtb = sb("dtb", [128, 64], F32)
        Aneg = sb("Aneg", [128, 64], F32)
        Dsk = sb("Dsk", [128, 64], F32)
        ss = sb("ss", [128, 8], F32)
        rstd = sb("rstd", [128, 8], F32)
        junk = sb("junk", [128, D], BF16)
        xs = [sb("xs0", [128, D], BF16)]
        bar_scr = sb("bar_scr", [128, 8], F32)
        Sst = [sb("Sst0", [128, 2, 512], F32)]
        Ssm = [sb("Ssm0", [128, 512], F32)]
        xpre = [sb("xpre0", [128, 3 + NT_MAX * 128 + 3], F32)]
        stio = [sb("stio%d" % i, [128, 128], F32) for i in range(2)]
        ARENA_WORDS = 17200
        arena = sb("arena", [128, ARENA_WORDS], F32)
        ar = {"off": 0}

        def A(shape, dt):
            n = int(np.prod(shape))
            nw = (n * (4 if dt == F32 else 2) + 3) // 4
            off = ar["off"]
            ar["off"] = off + nw
            assert ar["off"] <= ARENA_WORDS, ("arena overflow", ar["off"])
            ap = arena[:, off:off + nw]
            if dt == BF16:
                ap = ap.bitcast(BF16)[:, 0:n]
            if len(shape) == 2:
                ap = ap.rearrange("p (a b) -> p a b", a=shape[0])
            return ap

        psf = [st.enter_context(nc.psum_tensor("psf%d" % i, [128, 512], F32)) for i in range(8)]
        psbv = [p[:].bitcast(BF16) for p in psf]
        rr = {"f": 0, "b": 0, "n": 0}

        def PSF():
            i = rr["f"] % 8
            rr["f"] += 1
            return psf[i], ("psf", i)

        def PSB():
            i = rr["f"] % 8
            rr["f"] += 1
            return psbv[i], ("psf", i)

        def alt(lst, name):
            i = rr.get(name, 0)
            rr[name] = i + 1
            return lst[i % len(lst)], (name, i % len(lst))

        def mm(out, lhsT, rhs, start, stop, r, w):
            S.add("pe", lambda e: e.matmul(out=out, lhsT=lhsT, rhs=rhs, start=start, stop=stop), r, w)

        def tr(out, in_, ident, r, w):
            S.add("pe", lambda e: e.transpose(out=out, in_=in_, identity=ident), r, w)

        def tt(eng, out, in0, in1, op, r, w):
            S.add(eng, lambda e: e.tensor_tensor(out=out, in0=in0, in1=in1, op=op), r, w)

        def ts(eng, out, in0, s1, s2, op0, op1, r, w):
            S.add(eng, lambda e: e.tensor_scalar(out=out, in0=in0, scalar1=s1, scalar2=s2, op0=op0, op1=op1), r, w)

        def tss(eng, out, in_, s, op, r, w):
            S.add(eng, lambda e: e.tensor_single_scalar(out=out, in_=in_, scalar=s, op=op), r, w)

        def stt(eng, out, in0, scalar, in1, op0, op1, r, w):
            S.add(eng, lambda e: e.scalar_tensor_tensor(out=out, in0=in0, scalar=scalar, in1=in1, op0=op0, op1=op1), r, w)

        def cp(eng, out, in_, r, w):
            if eng == "act":
                S.add("act", lambda e: e.activation(out=out, in_=in_, func=AF.Copy), r, w)
            else:
                S.add(eng, lambda e: e.tensor_copy(out=out, in_=in_), r, w)

        def act(out, in_, func, r, w, bias=None, scale=None, accum=None):
            kw = {}
            if bias is not None:
                kw["bias"] = bias
            if scale is not None:
                kw["scale"] = scale
            if accum is not None:
                kw["accum_out"] = accum
            S.add("act", lambda e: e.activation(out=out, in_=in_, func=func, **kw), r, w)

        def rstd_act(dst, src, rkeys, wkeys, dn):
            act(dst, src, AF.Ln, rkeys, wkeys, scale=1.0 / dn, bias=EPS)
            act(dst, dst, AF.Exp, wkeys, wkeys, scale=-0.5)

        def memset(eng, ap, val, w):
            S.add(eng, lambda e: e.memset(ap, val), (), w)

        def dma(q, out, in_, r, w, slow=False):
            if slow:
                S.add(q, lambda e: e.dma_start(out=out, in_=in_, allow_slow_non_contiguous=True), r, w, dma=True)
            else:
                S.add(q, lambda e: e.dma_start(out=out, in_=in_), r, w, dma=True)

        def xk(t):
            return [("xres", t, n) for n in range(4)]

        def hk(n0, nn):
            return [("hT", t) for t in range(n0 // 128, (n0 + nn + 127) // 128)]

        BARK = [("bar", e_) for e_ in ("pe", "act", "dve", "pool")]

        def barrier_fn(_):
            p, pk = PSF()
            mm(p[0:1, 0:1], identb[0:1, 0:1], identb[0:1, 0:1], True, True, ["identb"], [pk, ("bar", "pe")])
            S.add("act", lambda e: e.activation(out=bar_scr[:, 0:1], in_=bar_scr[:, 4:5], func=AF.Copy), (), [("bar", "act")])
            S.add("dve", lambda e: e.memset(bar_scr[:, 1:2], 0.0), (), [("bar", "dve")])
            S.add("pool", lambda e: e.memset(bar_scr[:, 2:3], 0.0), (), [("bar", "pool")])
            S.add("act", lambda e: e.activation(out=bar_scr[:, 5:6], in_=bar_scr[:, 4:5], func=AF.Copy), BARK, [("bar2", "act")])
            S.add("dve", lambda e: e.memset(bar_scr[:, 6:7], 0.0), BARK, [("bar2", "dve")])
            S.add("pool", lambda e: e.memset(bar_scr[:, 7:8], 0.0), BARK, [("bar2", "pool")])

        memset("dve", bar_scr[:], 0.0, BARK + [("bar2", e_) for e_ in ("act", "dve", "pool")])
        dma("sp", identb[:], c_identb, (), ["identb"])
        dma("sp", identf[:], c_identf, (), ["identf"])
        dma("sp", Umat[:], c_U, (), ["Umat"])
        dma("sp", ones[:], c_ones, (), ["ones"])
        dma("sp", Ub[:], c_Ub, (), ["Ub"])
        dma("sp", onesb[:], c_onesb, (), ["onesb"])
        dma("sp", rmask[:], c_mask, (), ["rmask"])
        dma("sp", qdec[:], c_qdec, (), ["qdec"])
        dma("sp", kdec[:], c_kdec, (), ["kdec"])
        dma("sp", wcol[:], wcol_h, (), ["wcol"])
        dma("sp", nwcol[:], nwcol_h, (), ["nwcol"])
        dma("sp", cw[:], cw_h, (), ["cw"])
        dma("sp", cb[:], cb_h, (), ["cb"])
        dma("sp", dtb[:], dtb_h, (), ["dtb"])
        dma("sp", Aneg[:], alog_h, (), ["Aneg"])
        dma("sp", Dsk[:], dsk_h, (), ["Dsk"])
        act(Aneg[:], Aneg[:], AF.Exp, ["Aneg"], ["Aneg"])
        tss("dve", Aneg[:], Aneg[:], -1.0, ALU.mult, ["Aneg"], ["Aneg"])

        T = []

        def rms_to_hT(tiles, widx):
            def fn(_):
                nt_ = len(tiles)
                sk = [("ss", t) for t in range(nt_)]
                rk = [("rstd", t) for t in range(nt_)]
                memset("dve", ss[:, 0:nt_], 0.0, sk)
                for t in range(nt_):
                    act(junk[:], xres[:, t, :], AF.Square, xk(t) + [("ss", t)], [("xs", 1), ("ss", t)],
                        accum=ss[:, t:t + 1])
                rstd_act(rstd[:, 0:nt_], ss[:, 0:nt_], sk, rk, D)
                xsl = [(xs[0], ("xs", 0)), (junk, ("xs", 1))]
                for t in range(nt_):
                    xsb, xskey = xsl[t % 2]
                    act(xsb[:], xres[:, t, :], AF.Identity, xk(t) + [("rstd", t)], [xskey], scale=rstd[:, t:t + 1])
                    for hb in range(2):
                        p, pk = PSB()
                        pv = p[:].rearrange("p (a b) -> p a b", a=8)
                        for kk in range(8):
                            k = hb * 8 + kk
                            tr(pv[:, kk, :], xsb[:, k * 128:(k + 1) * 128], identb[:], [xskey, "identb"], [pk])
                        tt("dve", hT[:, hb * 8:(hb + 1) * 8, t * 128:(t + 1) * 128], pv,
                           wcol[:, widx, hb * 8:(hb + 1) * 8].unsqueeze(2).to_broadcast([128, 8, 128]),
                           ALU.mult, [pk, "wcol"], [("hT", t)])
            return fn

        def fm_proj(slab, ncols_chunks, ntl, evac):
            for (n0, nn) in ntl:
                for c in range(ncols_chunks):
                    p, pk = PSF()
                    for k in range(KC):
                        mm(p[:, 0:nn], slab[:, k, c * 128:(c + 1) * 128], hT[:, k, n0:n0 + nn],
                           k == 0, k == KC - 1, hk(n0, nn) + [slab_key[0]], [pk])
                    evac(c, n0, nn, p, pk)

        def fm_proj_pair(slab, ntl, evac):
            for (n0, nn) in ntl:
                ps_ = []
                for c in range(2):
                    p, pk = PSF()
                    for k in range(KC):
                        mm(p[:, 0:nn], slab[:, k, c * 128:(c + 1) * 128], hT[:, k, n0:n0 + nn],
                           k == 0, k == KC - 1, hk(n0, nn) + [slab_key[0]], [pk])
                    ps_.append((p, pk))
                evac(n0, nn, ps_[0][0], ps_[0][1], ps_[1][0], ps_[1][1])

        def tm_proj(slab, ncols, nt, evac):
            for t in range(nt):
                p, pk = PSF()
                for k in range(KC):
                    mm(p[:, 0:ncols], hT[:, k, t * 128:(t + 1) * 128], slab[:, k, 0:ncols],
                       k == 0, k == KC - 1, [("hT", t), slab_key[0]], [pk])
                evac(t, p, pk)

        def outproj_acc(lhsT4, lkey, wslab, t):
            for n in range(4):
                p, pk = PSF()
                for c in range(4):
                    mm(p[:], lhsT4[:, c, :], wslab[:, c, n * 512:(n + 1) * 512], c == 0, c == 3,
                       [lkey, slab_key[0]], [pk])
                tt("dve", xres[:, t, n * 512:(n + 1) * 512], p[:], xres[:, t, n * 512:(n + 1) * 512], ALU.add,
                   [pk, ("xres", t, n)], [("xres", t, n)])

        slab_key = [None]

        def fm_keys(pfx, ntl):
            return [(pfx, c, n0) for c in range(2) for (n0, _) in ntl]

        def retention_layer(u, tiles):
            nt = len(tiles)
            ntl = ntiles_of(tiles)
            last_unit = (u == len(UNITS) - 1)
            ar["off"] = 0
            qT = A([2, NTT], BF16)
            qTd = A([2, NTT], BF16)
            kT = A([2, NTT], BF16)
            k_tm = A([NT_MAX, DK], BF16)
            vv = A([NT_MAX, DV], BF16)
            sg = A([NT_MAX, DV], BF16)
            rt = [A([512], F32) for _ in range(4)]
            Pm = [A([128], BF16) for _ in range(NT_MAX)]
            yg = [A([512], BF16) for _ in range(NT_MAX)]
            ygT = [A([4, 128], BF16) for _ in range(NT_MAX)]
            Sbf = [A([2, 512], BF16) for _ in range(NT_MAX)]
            S2b = A([2, 512], F32)

            def rotary_evac(dst, pfx):
                def ev(n0, nn, p0, k0, p1, k1):
                    c_ = cosT[:, n0:n0 + nn]
                    s_ = sinT[:, n0:n0 + nn]
                    tt("dve", rt[0][:, 0:nn], p0[:, 0:nn], c_, ALU.mult, [k0, "cos"], ["rt0"])
                    tt("dve", rt[1][:, 0:nn], p1[:, 0:nn], s_, ALU.mult, [k1, "cos"], ["rt1"])
                    tt("pool", dst[:, 0, n0:n0 + nn], rt[0][:, 0:nn], rt[1][:, 0:nn], ALU.subtract,
                       ["rt0", "rt1"], [(pfx, 0, n0)])
                    tt("dve", rt[2][:, 0:nn], p1[:, 0:nn], c_, ALU.mult, [k1, "cos"], ["rt2"])
                    tt("dve", rt[3][:, 0:nn], p0[:, 0:nn], s_, ALU.mult, [k0, "cos"], ["rt3"])
                    tt("pool", dst[:, 1, n0:n0 + nn], rt[2][:, 0:nn], rt[3][:, 0:nn], ALU.add,
                       ["rt2", "rt3"], [(pfx, 1, n0)])
                return ev

            T.append((None, rms_to_hT(tiles, 0)))
            T.append((None, barrier_fn))
            for h in range(H):
                def fq(slab, h=h):
                    fm_proj_pair(slab, ntl, rotary_evac(qT, "qT"))
                    for c in range(2):
                        tt("pool", qTd[:, c, 0:nt * 128].rearrange("p (t i) -> p t i", i=128),
                           qT[:, c, 0:nt * 128].rearrange("p (t i) -> p t i", i=128),
                           qdec[:, h, :].unsqueeze(1).to_broadcast([128, nt, 128]), ALU.mult,
                           fm_keys("qT", ntl) + ["qdec"], [("qTd", c)])

                def fk(slab, h=h):
                    fm_proj_pair(slab, ntl, rotary_evac(kT, "kT"))
                    for t in range(nt):
                        p, pk = PSB()
                        for c in range(2):
                            tr(p[:, c * 128:(c + 1) * 128], kT[:, c, t * 128:(t + 1) * 128], identb[:],
                               fm_keys("kT", ntl) + ["identb"], [pk])
                        kd = kdec[:, 1 if tiles[t][0] == "s" else 0, h:h + 1]
                        tss("dve", k_tm[:, t, :], p[:, 0:256], kd, ALU.mult, [pk, "kdec"], [("k_tm", t)])
                def fqk(slab, fq=fq, fk=fk):
                    fq(slab[:, :, 0:DK])
                    fk(slab[:, :, DK:2 * DK])
                T.append((([(ret_w_in[:, h * DK:(h + 1) * DK], 0, DK),
                            (ret_w_in[:, 2048 + h * DK:2048 + (h + 1) * DK], DK, DK)], 16, 2 * DK), fqk))

                def fv(slab, h=h):
                    tm_proj(slab, DV, nt, lambda t, p, pk: cp("act", vv[:, t, :], p[:], [pk], [("vv", t)]))
                T.append(((ret_w_in[:, 4096 + h * DV:4096 + (h + 1) * DV], 16, DV), fv))

                def fg(slab, h=h):
                    tm_proj(slab, DV, nt, lambda t, p, pk: act(sg[:, t, :], p[:], AF.Silu, [pk], [("sg", t)]))
                T.append(((ret_w_in[:, 8192 + h * DV:8192 + (h + 1) * DV], 16, DV), fg))

                def fo(slab, h=h):
                    chains = []
                    pt = [t for t in range(nt) if tiles[t][0] == "p"]
                    stl = [t for t in range(nt) if tiles[t][0] == "s"]
                    if pt:
                        chains.append(("p", pt))
                    if stl:
                        chains.append(("s", stl))
                    allt = []
                    for ci, (kind, tl) in enumerate(chains):
                        n = len(tl)
                        base = len(allt)
                        Sx, skey = Sst[0], ("Sst", 0)
                        if kind == "p" and u == 0:
                            memset("dve", Sx[:], 0.0, [skey])
                        elif kind == "p":
                            dma("sp", Sx[:], scr_ret[h].rearrange("(a p) e -> p a e", p=128), ["scr_ret"], [skey])
                        else:
                            dma("sp", Sx[:], st_ret[h].rearrange("(a p) e -> p a e", p=128), (), [skey])
                        cdec = gam[h] ** (128.0 if kind == "p" else 64.0)
                        cp("act", Sbf[base][:], Sx[:], [skey], [("Sbf", base)])
                        sbufs = [(Sx, skey), (S2b, ("S2b", 0))]
                        for i, t in enumerate(tl):
                            allt.append((base + i, t))
                            (Si, sik), (So, sok) = sbufs[i % 2], sbufs[(i + 1) % 2]
                            for hf in range(2):
                                pS, pSk = PSF()
                                mm(pS[:], k_tm[:, t, hf * 128:(hf + 1) * 128], vv[:, t, :], True, True,
                                   [("k_tm", t), ("vv", t)], [pSk])
                                stt("dve", So[:, hf, :], Si[:, hf, :], cdec, pS[:], ALU.mult, ALU.add,
                                    [sik, pSk], [sok])
                            if i + 1 < n:
                                cp("act", Sbf[base + i + 1][:], So[:], [sok], [("Sbf", base + i + 1)])
                        if n % 2 == 1:
                            cp("dve", Sx[:], S2b[:], [("S2b", 0)], [skey])
                        if kind == "s":
                            dst, dk_ = o_ret_s[h], ["o_ret_s"]
                        elif last_unit:
                            dst, dk_ = o_ret_p[h], ["o_ret_p"]
                        else:
                            dst, dk_ = scr_ret[h], ["scr_ret"]
                        dma("sp", dst.rearrange("(a p) e -> p a e", p=128), Sx[:], [skey], dk_)
                    for i, t in allt:
                        tsl = slice(t * 128, (t + 1) * 128)
                        p, pk = PSF()
                        for c in range(2):
                            mm(p[:, 0:128], kT[:, c, tsl], qT[:, c, tsl], c == 0, c == 1,
                               fm_keys("kT", ntl) + fm_keys("qT", ntl), [pk])
                        tt("dve", Pm[i][:], p[:, 0:128], rmask[:, h, :], ALU.mult, [pk, "rmask"], [("Pm", i)])
                    for i, t in allt:
                        tsl = slice(t * 128, (t + 1) * 128)
                        py, pyk = PSF()
                        mm(py[:], Pm[i][:], vv[:, t, :], True, False, [("Pm", i), ("vv", t)], [pyk])
                        mm(py[:], qTd[:, 0, tsl], Sbf[i][:, 0, :], False, False, [("qTd", 0), ("Sbf", i)], [pyk])
                        mm(py[:], qTd[:, 1, tsl], Sbf[i][:, 1, :], False, True, [("qTd", 1), ("Sbf", i)], [pyk])
                        memset("dve", ss[:, i:i + 1], 0.0, [("ss", i)])
                        act(junk[:, 0:512], py[:], AF.Square, [pyk, ("ss", i)], [("xs", 1), ("ss", i)],
                            accum=ss[:, i:i + 1])
                        rstd_act(rstd[:, i:i + 1], ss[:, i:i + 1], [("ss", i)], [("rstd", i)], DV)
                        stt("dve", yg[i][:], py[:], rstd[:, i:i + 1], sg[:, t, :], ALU.mult, ALU.mult,
                            [pyk, ("rstd", i), ("sg", t)], [("yg", i)])
                    for i, t in allt:
                        pT, pTk = PSB()
                        pTv = pT[:, 0:512].rearrange("p (a b) -> p a b", a=4)
                        for c in range(4):
                            tr(pTv[:, c, :], yg[i][:, c * 128:(c + 1) * 128], identb[:], [("yg", i), "identb"], [pTk])
                        cp("act", ygT[i][:], pTv, [pTk], [("ygT", i)])
                    for i, t in allt:
                        outproj_acc(ygT[i], ("ygT", i), slab, t)
                T.append(((ret_w_out[h * DV:(h + 1) * DV, :], 4, D), fo))

        def ffn_layer(u, tiles, li):
            nt = len(tiles)
            ntl = ntiles_of(tiles)
            ar["off"] = 0
            sgT = A([4, NTT], BF16)
            aT = [A([4, NTT], BF16) for _ in range(2)]
            T.append((None, rms_to_hT(tiles, 2 + li)))
            T.append((None, barrier_fn))
            for j in range(DFF // 512):
                def fgate(slab, j=j):
                    fm_proj(slab, 4, ntl, lambda c, n0, nn, p, pk:
                            act(sgT[:, c, n0:n0 + nn], p[:, 0:nn], AF.Silu, [pk], [("sgT", c, n0)]))
                T.append(((w_gate[li][:, j * 512:(j + 1) * 512], 16, 512), fgate))

                def fup(slab, j=j):
                    ab, abk = alt(aT, "aT")
                    fm_proj(slab, 4, ntl, lambda c, n0, nn, p, pk:
                            tt("dve", ab[:, c, n0:n0 + nn], p[:, 0:nn], sgT[:, c, n0:n0 + nn], ALU.mult,
                               [pk, ("sgT", c, n0)], [abk]))
                    rr["cur_aT"] = (ab, abk)
                T.append(((w_up[li][:, j * 512:(j + 1) * 512], 16, 512), fup))

                def fdown(slab, j=j):
                    ab, abk = rr["cur_aT"]
                    for t in range(nt):
                        outproj_acc(ab[:, :, t * 128:(t + 1) * 128], abk, slab, t)
                T.append(((w_down[li][j * 512:(j + 1) * 512, :], 4, D), fdown))

        def ssd_layer(u, tiles):
            nt = len(tiles)
            ntl = ntiles_of(tiles)
            last_unit = (u == len(UNITS) - 1)
            pt = [t for t in range(nt) if tiles[t][0] == "p"]
            stl = [t for t in range(nt) if tiles[t][0] == "s"]
            npr = len(pt) * 128
            regions = [("p", 0, 0, npr)]
            if stl:
                regions.append(("s", 3 + npr, npr, 128))

            def xcol(n0):
                return 3 + n0 if n0 < npr else 3 + npr + 3 + (n0 - npr)

            ar["off"] = 0
            xcT = A([4, NTT], F32)
            BT = A([NTT], BF16)
            CT = A([NTT], BF16)
            dtt = A([NT_MAX, 64], F32)
            aa = A([NT_MAX, 64], F32)
            a3 = [A([NT_MAX, 64], BF16) for _ in range(3)]
            ares = [A([64], F32) for _ in range(2)]
            cum = A([NT_MAX, 64], F32)
            ecum = A([NT_MAX, 64], F32)
            dend = A([NT_MAX, 64], F32)
            etot = A([NT_MAX, 64], F32)
            sg = A([NT_MAX, DV], BF16)
            cacc = [A([NTT], F32)]
            tmp64 = [A([64], F32) for _ in range(2)]
            x_tm = [A([512], F32) for _ in range(2)]
            xdt = [A([512], BF16) for _ in range(2)]
            xdt2 = [A([512], BF16) for _ in range(2)]
            B_tm = [A([128], BF16) for _ in range(2)]
            CBm = [A([128], F32) for _ in range(2)]
            Lm = [A([4, 128], F32) for _ in range(2)]
            MT = [A([4, 128], BF16) for _ in range(4)]
            yf = [A([512], F32) for _ in range(3)]
            yg = [A([512], BF16) for _ in range(2)]
            ygT = [A([4, 128], BF16) for _ in range(2)]
            Ssb = [A([512], BF16)]

            T.append((None, rms_to_hT(tiles, 1)))
            T.append((None, barrier_fn))

            def fdt(slab):
                def ev(t, p, pk):
                    tb, tbk = alt(tmp64, "tmp64")
                    tt("dve", tb[:], p[:, 0:64], dtb[:], ALU.add, [pk, "dtb"], [tbk])
                    if DBG.get("dtl", 9) < 1:
                        return
                    act(tb[:], tb[:], AF.Exp, [tbk], [tbk])
                    act(dtt[:, t, :], tb[:], AF.Ln, [tbk], [("dtt", t)], bias=1.0)
                    if tiles[t][0] == "s":
                        memset("dve", dtt[64:128, t, :], 0.0, [("dtt", t)])
                    if DBG.get("dtl", 9) < 2:
                        return
                    tt("dve", aa[:, t, :], dtt[:, t, :], Aneg[:], ALU.mult, [("dtt", t), "Aneg"], [("aa", t)])
                    if DBG.get("dtl", 9) < 3:
                        return
                    cp("dve", a3[0][:, t, :], aa[:, t, :], [("aa", t)], [("a3", t)])
                    tt("dve", ares[0][:], aa[:, t, :], a3[0][:, t, :], ALU.subtract, [("aa", t), ("a3", t)], ["ares0"])
                    cp("dve", a3[1][:, t, :], ares[0][:], ["ares0"], [("a3", t)])
                    tt("dve", ares[1][:], ares[0][:], a3[1][:, t, :], ALU.subtract, ["ares0", ("a3", t)], ["ares1"])
                    cp("dve", a3[2][:, t, :], ares[1][:], ["ares1"], [("a3", t)])
                    if DBG.get("dtl2", 9) < 1:
                        return
                    pc, pck = PSF()
                    for q3 in range(3):
                        mm(pc[:, 0:64], Ub[:], a3[q3][:, t, :], q3 == 0, q3 == 2, ["Ub", ("a3", t)], [pck])
                    if DBG.get("dtl2", 9) < 2:
                        return
                    cp("dve", cum[:, t, :], pc[:, 0:64], [pck], [("cum", t)])
                    tss("dve", aa[:, t, :], cum[:, t, :], -1.0, ALU.mult, [("cum", t)], [("aa", t)])
                    if DBG.get("dtl2", 9) < 3:
                        return
                    act(ecum[:, t, :], cum[:, t, :], AF.Exp, [("cum", t)], [("ecum", t)])
                    if DBG.get("dtl", 9) < 4:
                        return
                    pt_, ptk = PSF()
                    for q3 in range(3):
                        mm(pt_[:, 0:64], onesb[:], a3[q3][:, t, :], q3 == 0, q3 == 2, ["onesb", ("a3", t)], [ptk])
                    tb2, tb2k = alt(tmp64, "tmp64")
                    tt("dve", tb2[:], pt_[:, 0:64], cum[:, t, :], ALU.subtract, [ptk, ("cum", t)], [tb2k])
                    cp("dve", etot[:, t, :], pt_[:, 0:64], [ptk], [("etot", t)])
                    act(etot[:, t, :], etot[:, t, :], AF.Exp, [("etot", t)], [("etot", t)])
                    act(dend[:, t, :], tb2[:], AF.Exp, [tb2k], [("dend", t)])
                tm_proj(slab, 64, nt, ev)
            if "dt" in DBG["ssd_parts"]:
                T.append(((ssd_w_in[:, 10240:10304], 16, 64), fdt))

            def conv_chunk(chidx, pre, prek, dst_fn, dkey_fn, silu_dt_note):
                for (kind, c0, tok0, ntok) in regions:
                    ca, cak = alt(cacc, "cacc")
                    ts("dve", ca[:, 0:ntok], pre[:, c0 + 3:c0 + 3 + ntok], cw[:, chidx, 3:4], cb[:, chidx:chidx + 1],
                       ALU.mult, ALU.add, [prek, "cw", "cb"], [cak])
                    for tp in range(3):
                        stt("dve", ca[:, 0:ntok], pre[:, c0 + tp:c0 + tp + ntok],
                            cw[:, chidx, tp:tp + 1], ca[:, 0:ntok], ALU.mult, ALU.add, [prek, cak, "cw"], [cak])
                    act(dst_fn(tok0, ntok), ca[:, 0:ntok], AF.Silu, [cak], [dkey_fn(tok0)])

            def conv_state_io(chidx, pre, prek):
                ch0 = chidx * 128
                for (kind, c0, tok0, ntok) in regions:
                    if kind == "p" and u == 0:
                        memset("dve", pre[:, c0:c0 + 3], 0.0, [prek])
                    elif kind == "p":
                        dma("sp", pre[:, c0:c0 + 3], scr_conv[ch0:ch0 + 128, :], ["scr_conv"], [prek])
                    else:
                        dma("sp", pre[:, c0:c0 + 3], st_conv[ch0:ch0 + 128, :], (), [prek])

            def conv_state_out(chidx, pre, prek):
                ch0 = chidx * 128
                for (kind, c0, tok0, ntok) in regions:
                    nvalid = ntok if kind == "p" else 64
                    src = pre[:, c0 + nvalid:c0 + nvalid + 3]
                    if kind == "s":
                        dma("sp", o_conv_s[ch0:ch0 + 128, :], src, [prek], ["o_conv_s"])
                    elif last_unit:
                        dma("sp", o_conv_p[ch0:ch0 + 128, :], src, [prek], ["o_conv_p"])
                    else:
                        dma("sp", scr_conv[ch0:ch0 + 128, :], src, [prek], ["scr_conv"])

            def fm_conv_proj(slab, nchunks, chidx0, dst_of_chunk, dkey_of_chunk):
                for c in range(nchunks):
                    pre, prek = alt(xpre, "xpre")
                    conv_state_io(chidx0 + c, pre, prek)
                    for (n0, nn) in ntl:
                        p, pk = PSF()
                        for k in range(KC):
                            mm(p[:, 0:nn], slab[:, k, c * 128:(c + 1) * 128], hT[:, k, n0:n0 + nn],
                               k == 0, k == KC - 1, hk(n0, nn) + [slab_key[0]], [pk])
                        cp("act", pre[:, xcol(n0):xcol(n0) + nn], p[:, 0:nn], [pk], [prek])
                    conv_state_out(chidx0 + c, pre, prek)
                    conv_chunk(chidx0 + c, pre, prek, dst_of_chunk(c), dkey_of_chunk(c), None)

            for g in range(NG):
                def fx(slab, g=g):
                    fm_conv_proj(slab, 4, g * 4,
                                 lambda c: (lambda tok0, ntok: xcT[:, c, tok0:tok0 + ntok]),
                                 lambda c: (lambda tok0: ("xcT", c, tok0)))
                if "x" in DBG["ssd_parts"]:
                    T.append(((ssd_w_in[:, 4096 + g * 512:4096 + (g + 1) * 512], 16, 512), fx))

                def fB(slab, g=g):
                    fm_conv_proj(slab, 1, 32 + g,
                                 lambda c: (lambda tok0, ntok: BT[:, tok0:tok0 + ntok]),
                                 lambda c: (lambda tok0: ("BT", tok0)))

                def fC(slab, g=g):
                    fm_conv_proj(slab, 1, 40 + g,
                                 lambda c: (lambda tok0, ntok: CT[:, tok0:tok0 + ntok]),
                                 lambda c: (lambda tok0: ("CT", tok0)))

                def fBC(slab, fB=fB, fC=fC):
                    fB(slab[:, :, 0:128])
                    fC(slab[:, :, 128:256])
                T.append((([(ssd_w_in[:, 8192 + g * 128:8192 + (g + 1) * 128], 0, 128),
                            (ssd_w_in[:, 9216 + g * 128:9216 + (g + 1) * 128], 128, 128)], 16, 256), fBC))

                def fz(slab, g=g):
                    tm_proj(slab, 512, nt, lambda t, p, pk: act(sg[:, t, :], p[:], AF.Silu, [pk], [("sg", t)]))
                if "z" in DBG["ssd_parts"]:
                    T.append(((ssd_w_in[:, g * 512:(g + 1) * 512], 16, 512), fz))

                def fscan(slab, g=g):
                    chains = []
                    if pt:
                        chains.append(("p", pt))
                    if stl:
                        chains.append(("s", stl))
                    rk_x = [("xcT", c, r[2]) for c in range(4) for r in regions]
                    rk_B = [("BT", r[2]) for r in regions]
                    rk_C = [("CT", r[2]) for r in regions]
                    hs = slice(g * 8, (g + 1) * 8)
                    h8 = lambda ap: ap.rearrange("p (h q) -> p h q", h=8)
                    Sx, Sb_ = Ssm[0], Ssb[0]
                    skey, sbkey = ("Ssm", 0), ("Ssb", 0)
                    order = []
                    for kind, tl in reversed(chains):
                        for j_, t_ in enumerate(tl):
                            order.append((t_, kind, j_ == 0, j_ == len(tl) - 1))
                    n = len(order)

                    def chain_init(kind):
                        if kind == "p" and u == 0:
                            memset("dve", Sx[:], 0.0, [skey])
                        elif kind == "p":
                            dma("sp", Sx[:], scr_ssm[g], ["scr_ssm"], [skey])
                        else:
                            for c4 in range(4):
                                sio, siok = alt(stio, "stio")
                                r0 = g * 512 + c4 * 128
                                dma("sp", sio[:], st_ssm[r0:r0 + 128, :], (), [siok])
                                p, pk = PSF()
                                tr(p[:, 0:128], sio[:], identf[:], [siok, "identf"], [pk])
                                cp("act", Sx[:, c4 * 128:(c4 + 1) * 128], p[:, 0:128], [pk], [skey])
                        cp("act", Sb_[:], Sx[:], [skey], [sbkey])

                    def chain_end(kind):
                        if kind == "p" and not last_unit:
                            dma("sp", scr_ssm[g], Sx[:], [skey], ["scr_ssm"])
                        else:
                            dst = o_ssm_s if kind == "s" else o_ssm_p
                            dkk = "o_ssm_s" if kind == "s" else "o_ssm_p"
                            for c4 in range(4):
                                p, pk = PSF()
                                tr(p[:, 0:128], Sx[:, c4 * 128:(c4 + 1) * 128], identf[:], [skey, "identf"], [pk])
                                sio, siok = alt(stio, "stio")
                                cp("act", sio[:], p[:, 0:128], [pk], [siok])
                                r0 = g * 512 + c4 * 128
                                dma("sp", dst[r0:r0 + 128, :], sio[:], [siok], [dkk])

                    if True:
                        def S1(i, t):
                            par = i % 2
                            tsl = slice(t * 128, (t + 1) * 128)
                            px, pxk = PSF()
                            for c in range(4):
                                tr(px[:, c * 128:(c + 1) * 128], xcT[:, c, tsl], identf[:], rk_x + ["identf"], [pxk])
                            cp("act", x_tm[par][:], px[:], [pxk], [("x_tm", par)])
                            tt("dve", h8(xdt[par][:]), h8(x_tm[par][:]),
                               dtt[:, t, hs].unsqueeze(2).to_broadcast([128, 8, 64]), ALU.mult,
                               [("x_tm", par), ("dtt", t)], [("xdt", par)])
                            tt("pool", h8(xdt2[par][:]), h8(xdt[par][:]),
                               dend[:, t, hs].unsqueeze(2).to_broadcast([128, 8, 64]), ALU.mult,
                               [("xdt", par), ("dend", t)], [("xdt2", par)])
                            pB, pBk = PSB()
                            tr(pB[:, 0:128], BT[:, tsl], identb[:], rk_B + ["identb"], [pBk])
                            cp("act", B_tm[par][:], pB[:, 0:128], [pBk], [("B_tm", par)])
                            pcb, pcbk = PSF()
                            mm(pcb[:, 0:128], BT[:, tsl], CT[:, tsl], True, True, rk_B + rk_C, [pcbk])
                            tt("dve", CBm[par][:], pcb[:, 0:128], Umat[:], ALU.mult, [pcbk, "Umat"], [("CBm", par)])
                            for hb in range(2):
                                pcr, pcrk = PSF()
                                for hh in range(4):
                                    hg = g * 8 + hb * 4 + hh
                                    for q3 in range(3):
                                        mm(pcr[:, hh * 128:(hh + 1) * 128], a3[q3][:, t, hg:hg + 1].to_broadcast([128, 128]),
                                           Ub[:], q3 == 0, q3 == 2, [("a3", t), "Ub"], [pcrk])
                                h0 = g * 8 + hb * 4
                                for hh in range(4):
                                    act(Lm[hb][:, hh, :], pcr[:, hh * 128:(hh + 1) * 128], AF.Exp, [pcrk, ("aa", t)], [("Lm", hb)],
                                        bias=aa[:, t, h0 + hh:h0 + hh + 1])
                                mi = par * 2 + hb
                                stt("dve", MT[mi][:], Lm[hb][:], 1.0, CBm[par][:].unsqueeze(1).to_broadcast([128, 4, 128]),
                                    ALU.min, ALU.mult, [("Lm", hb), ("CBm", par)], [("MT", mi)])

                        def S2(i, t):
                            par = i % 2
                            tsl = slice(t * 128, (t + 1) * 128)
                            pyi, pyik = PSF()
                            for hb in range(2):
                                mi = par * 2 + hb
                                for hh in range(4):
                                    hl = hb * 4 + hh
                                    mm(pyi[:, hl * 64:(hl + 1) * 64], MT[mi][:, hh, :], xdt[par][:, hl * 64:(hl + 1) * 64],
                                       True, True, [("MT", mi), ("xdt", par)], [pyik])
                            pyc, pyck = PSF()
                            mm(pyc[:], CT[:, tsl], Sb_[:], True, True, rk_C + [sbkey], [pyck])
                            pS, pSk = PSF()
                            mm(pS[:], B_tm[par][:], xdt2[par][:], True, True, [("B_tm", par), ("xdt2", par)], [pSk])
                            tt("dve", h8(Sx[:]), h8(Sx[:]), etot[:, t, hs].unsqueeze(2).to_broadcast([128, 8, 64]), ALU.mult,
                               [skey, ("etot", t)], [skey])
                            tt("dve", Sx[:], Sx[:], pS[:], ALU.add, [skey, pSk], [skey])
                            cp("act", Sb_[:], Sx[:], [skey], [sbkey])
                            y0, y0k = yf[par], ("yf", par)
                            y1, y1k = yf[2], ("yf", 2)
                            tt("dve", h8(y0[:]), h8(pyc[:]), ecum[:, t, hs].unsqueeze(2).to_broadcast([128, 8, 64]), ALU.mult,
                               [pyck, ("ecum", t)], [y0k])
                            tt("dve", y0[:], pyi[:], y0[:], ALU.add, [pyik, y0k], [y0k])
                            tt("pool", h8(y1[:]), h8(x_tm[par][:]), Dsk[:, hs].unsqueeze(2).to_broadcast([128, 8, 64]), ALU.mult,
                               [("x_tm", par), "Dsk"], [y1k])
                            tt("dve", y0[:], y0[:], y1[:], ALU.add, [y0k, y1k], [y0k])
                            tt("dve", y0[:], y0[:], sg[:, t, :], ALU.mult, [y0k, ("sg", t)], [y0k])
                            sc = 6 + par
                            memset("dve", ss[:, sc:sc + 1], 0.0, [("ss", sc)])
                            act(junk[:, 0:512], y0[:], AF.Square, [y0k, ("ss", sc)], [("xs", 1), ("ss", sc)], accum=ss[:, sc:sc + 1])
                            rstd_act(rstd[:, sc:sc + 1], ss[:, sc:sc + 1], [("ss", sc)], [("rstd", sc)], 512)
                            act(yg[par][:], y0[:], AF.Identity, [y0k, ("rstd", sc)], [("yg", par)], scale=rstd[:, sc:sc + 1])

                        def S3a(i, t):
                            par = i % 2
                            pT, pTk = PSB()
                            pTv = pT[:, 0:512].rearrange("p (a b) -> p a b", a=4)
                            for c in range(4):
                                tr(pTv[:, c, :], yg[par][:, c * 128:(c + 1) * 128], identb[:], [("yg", par), "identb"], [pTk])
                            tt("dve", ygT[par][:], pTv, nwcol[:, g * 4:(g + 1) * 4].unsqueeze(2).to_broadcast([128, 4, 128]),
                               ALU.mult, [pTk, "nwcol"], [("ygT", par)])

                        def S3b(i, t):
                            par = i % 2
                            outproj_acc(ygT[par], ("ygT", par), slab, t)

                        for s_ in range(n + 3):
                            if s_ < n:
                                S1(s_, order[s_][0])
                            if 0 <= s_ - 1 < n:
                                t_, kind_, first_, last_ = order[s_ - 1]
                                if first_:
                                    chain_init(kind_)
                                S2(s_ - 1, t_)
                                if last_:
                                    chain_end(kind_)
                            if 0 <= s_ - 2 < n:
                                S3a(s_ - 2, order[s_ - 2][0])
                            if 0 <= s_ - 3 < n:
                                S3b(s_ - 3, order[s_ - 3][0])
                if "scan" in DBG["ssd_parts"]:
                    T.append(((ssd_w_out[g * 512:(g + 1) * 512, :], 4, D), fscan))

        def tile_rows(kind, i):
            if kind == "p":
                return i * 128, 128
            return SEQ, DEC_SEQ

        for u, tiles in enumerate(UNITS):
            if u not in DBG["units"]:
                continue
            def fload(_, u=u, tiles=tiles):
                dma("sp", cosT[:], c_cos[u], (), ["cos"])
                dma("sp", sinT[:], c_sin[u], (), ["cos"])
                for t, (kind, i) in enumerate(tiles):
                    r0, nr = tile_rows(kind, i)
                    if nr < 128:
                        memset("pool", xres[:, t, :], 0.0, xk(t))
                    dma("sp", xres[0:nr, t, :], xin[r0:r0 + nr, :], (), xk(t))
            T.append((None, fload))
            if "ret" in DBG["phases"]:
                retention_layer(u, tiles)
            if "ffn0" in DBG["phases"]:
                ffn_layer(u, tiles, 0)
            if "ssd" in DBG["phases"]:
                ssd_layer(u, tiles)
            if "ffn1" in DBG["phases"]:
                ffn_layer(u, tiles, 1)

            def ffinal(_, u=u, tiles=tiles):
                barrier_fn(None)
                wf = arena[:, 0:D]
                dma("sp", wf, lnf_h, BARK, ["wfin"])
                for t, (kind, i) in enumerate(tiles):
                    memset("dve", ss[:, t:t + 1], 0.0, [("ss", t)])
                    act(junk[:], xres[:, t, :], AF.Square, xk(t) + [("ss", t)], [("xs", 1), ("ss", t)],
                        accum=ss[:, t:t + 1])
                    rstd_act(rstd[:, t:t + 1], ss[:, t:t + 1], [("ss", t)], [("rstd", t)], D)
                    stt("dve", xres[:, t, :], xres[:, t, :], rstd[:, t:t + 1], wf, ALU.mult, ALU.mult,
                        xk(t) + [("rstd", t), "wfin"], xk(t))
                    r0, nr = tile_rows(kind, i)
                    dma("sp", yout[r0:r0 + nr, :], xres[0:nr, t, :], xk(t), ["yout"])
            T.append((None, ffinal))

        slabs = [i for i, tk in enumerate(T) if tk[0] is not None]
        issued = [0]

        def issue_load(n):
            spec = T[slabs[n]][0]
            src, a, c = spec
            b = n % NWBUF
            view = wbuf[b][:, 0:a * c].rearrange("p (a c) -> p a c", a=a)
            srcs = src if isinstance(src, list) else [(src, 0, c)]
            for (sap, c0, nc_) in srcs:
                if a == 16:
                    srcv = sap.rearrange("(k p) n -> p k n", p=128)
                else:
                    srcv = sap.rearrange("(c p) n -> p c n", p=128)
                dv = view[:, :, c0:c0 + nc_]
                S.add("pool", lambda e, dv=dv, srcv=srcv: e.dma_start(out=dv, in_=srcv), (), [("w", b)], dma=True)

        ns = 0
        for i, (spec, fn) in enumerate(T):
            if spec is None:
                fn(None)
                continue
            while issued[0] < min(ns + NWBUF, len(slabs)):
                issue_load(issued[0])
                issued[0] += 1
            b = ns % NWBUF
            a, c = spec[1], spec[2]
            slab_key[0] = ("w", b)
            fn(wbuf[b][:, 0:a * c].rearrange("p (a c) -> p a c", a=a))
            ns += 1

        S.emit(nc)
    return nc, S


_CACHE = {}


def kernel(x_prompt, x_sample, state_ret, state_ssm, state_conv, ln_mix, ln_ffn, ln_final,
           ret_w_in, ret_w_out, ssd_w_in, ssd_conv_w, ssd_conv_b, ssd_dt_bias, ssd_a_log,
           ssd_d, ssd_norm_w, ssd_w_out, ffn_w_gate, ffn_w_up, ffn_w_down):
    f = lambda a: np.ascontiguousarray(np.asarray(a, dtype=np.float32))
    col = lambda v: np.asarray(v, dtype=np.float32).reshape(-1, 128).T
    if "nc" not in _CACHE:
        _CACHE["nc"] = build_program()[0]
    nc = _CACHE["nc"]
    mask, qdec, kdec, U = host_tables()
    cs = [host_consts(u) for u in range(len(UNITS))]
    c_cos = np.stack([c[0] for c in cs])
    c_sin = np.stack([c[1] for c in cs])
    shared = {
        "wcol_h": f(np.stack([col(ln_mix[0]), col(ln_mix[1]), col(ln_ffn[0]), col(ln_ffn[1]), col(ln_final)], axis=1)),
        "nwcol_h": f(col(ssd_norm_w[0])),
        "cw_h": f(np.stack([col(ssd_conv_w[0][tp]) for tp in range(4)], axis=2)),
        "cb_h": f(col(ssd_conv_b[0])),
        "dtb_h": f(np.broadcast_to(np.asarray(ssd_dt_bias[0]), (128, 64))),
        "alog_h": f(np.broadcast_to(np.asarray(ssd_a_log[0]), (128, 64))),
        "dsk_h": f(np.broadcast_to(np.asarray(ssd_d[0]), (128, 64))),
        "lnf_h": f(np.broadcast_to(np.asarray(ln_final), (128, D))),
        "ret_w_in": f(ret_w_in[0]), "ret_w_out": f(ret_w_out[0]), "ssd_w_in": f(ssd_w_in[0]),
        "ssd_w_out": f(ssd_w_out[0]), "w_gate": f(ffn_w_gate), "w_up": f(ffn_w_up), "w_down": f(ffn_w_down),
        "c_identb": np.eye(128, dtype=np.float32).astype(ml_dtypes.bfloat16),
        "c_identf": np.eye(128, dtype=np.float32), "c_U": U, "c_ones": np.ones((128, 128), np.float32),
        "c_Ub": U.astype(ml_dtypes.bfloat16), "c_onesb": np.ones((128, 128), np.float32).astype(ml_dtypes.bfloat16),
        "c_mask": mask, "c_qdec": qdec, "c_kdec": kdec, "c_cos": c_cos, "c_sin": c_sin,
    }
    xp, xs_ = f(x_prompt), f(x_sample)
    in_maps = []
    for c in range(8):
        m = dict(shared)
        m["xin"] = np.concatenate([xp[c % 4], xs_[c]], axis=0)
        m["st_ret"] = f(state_ret[0, c])
        m["st_ssm"] = f(state_ssm[0, c]).reshape(64 * 64, 128)
        m["st_conv"] = f(np.asarray(state_conv[0, c]).T)
        in_maps.append(m)
    ncr = DBG["ncores"]
    res = run_bass_kernel_spmd(nc, in_maps[:ncr], core_ids=list(range(ncr)))
    R = list(res.results)
    while len(R) < 8:
        R.append({k: np.zeros_like(v) for k, v in R[0].items()})
    y_prompt = np.stack([R[c]["yout"][:SEQ] for c in range(4)])
    y_sample = np.stack([R[c]["yout"][SEQ:] for c in range(8)])
    ret_p = np.stack([R[c]["o_ret_p"] for c in range(4)])[None]
    ssm_p = np.stack([R[c]["o_ssm_p"].reshape(64, 64, 128) for c in range(4)])[None]
    conv_p = np.stack([R[c]["o_conv_p"].T for c in range(4)])[None]
    ret_s = np.stack([R[c]["o_ret_s"] for c in range(8)])[None]
    ssm_s = np.stack([R[c]["o_ssm_s"].reshape(64, 64, 128) for c in range(8)])[None]
    conv_s = np.stack([R[c]["o_conv_s"].T for c in range(8)])[None]
    return tuple(np.ascontiguousarray(a, dtype=np.float32) for a in
                 (y_prompt, y_sample, ret_p, ssm_p, conv_p, ret_s, ssm_s, conv_s))
```
